# Optimizing a Trainium2 kernel written in Bass

```python
import math
import jax, jax.numpy as jnp
from jax import lax
import numpy as np

D_MODEL = 2048
BATCH = 1
SEQ = 16384
DEPTH = 4

MIX_WIDTH = D_MODEL
GMLP_WIDTH = D_MODEL // 4
GMLP_GROUPS = 4
GMLP_GROUP_DIM = GMLP_WIDTH // GMLP_GROUPS
GMLP_CHUNK = 128
DIFF_WIDTH = 3 * D_MODEL // 8
DIFF_HEAD_DIM = 64
DIFF_HEADS = DIFF_WIDTH // (2 * DIFF_HEAD_DIM)
ATTN_BLOCK = 128
GLA_WIDTH = MIX_WIDTH - GMLP_WIDTH - DIFF_WIDTH
GLA_HEADS = 4
GLA_KEY_WIDTH = GLA_WIDTH // 2
GLA_DK = GLA_KEY_WIDTH // GLA_HEADS
GLA_DV = GLA_WIDTH // GLA_HEADS
GLA_GATE_RANK = 16
GLA_TAU = 16.0
GLA_CHUNK = 64
FFN_HIDDEN = -(-8 * D_MODEL // (3 * 256)) * 256
EPS = 1e-6

IN_SPLITS = [GMLP_WIDTH, GMLP_WIDTH,
             DIFF_WIDTH, DIFF_WIDTH, DIFF_WIDTH,
             GLA_KEY_WIDTH, GLA_KEY_WIDTH,
             GLA_WIDTH, GLA_WIDTH,
             GLA_GATE_RANK]
IN_WIDTH = sum(IN_SPLITS)
IN_SPLIT_IDX = list(np.cumsum(IN_SPLITS)[:-1].tolist())

kernel_name = "hymba_style_gmlp_diffattn_gla_hybrid"


def rmsnorm(x, g):
    xf = x.astype(jnp.float32)
    y = xf * lax.rsqrt(jnp.mean(xf * xf, axis=-1, keepdims=True) + EPS) * g.astype(jnp.float32)
    return y.astype(x.dtype)


def layernorm(x, g):
    xf = x.astype(jnp.float32)
    mu = jnp.mean(xf, axis=-1, keepdims=True)
    var = jnp.mean(jnp.square(xf - mu), axis=-1, keepdims=True)
    return ((xf - mu) * lax.rsqrt(var + EPS) * g.astype(jnp.float32)).astype(x.dtype)


def gmlp_mixer(u, v, ln_g, spatial_w, spatial_b):
    u = jax.nn.gelu(u)
    v = layernorm(jax.nn.gelu(v), ln_g)
    B, T, _ = v.shape
    n = T // GMLP_CHUNK
    vc = v.reshape(B, n, GMLP_CHUNK, GMLP_GROUPS, GMLP_GROUP_DIM)
    causal = jnp.tril(jnp.ones((GMLP_CHUNK, GMLP_CHUNK), dtype=bool))
    w = jnp.where(causal[None], spatial_w, jnp.zeros_like(spatial_w))
    mixed = jnp.einsum('gts,bnsgc->bntgc', w, vc) + spatial_b.T[:, :, None]
    return u * mixed.reshape(B, T, GMLP_WIDTH)


def diff_attention(q, k, v, lambdas, norm_g, lambda_init):
    B, T, _ = q.shape
    q = q.reshape(B, T, DIFF_HEADS, 2, DIFF_HEAD_DIM)
    k = k.reshape(B, T, DIFF_HEADS, 2, DIFF_HEAD_DIM)
    v = v.reshape(B, T, DIFF_HEADS, 2 * DIFF_HEAD_DIM)
    lf = lambdas.astype(jnp.float32)
    lam = jnp.exp(jnp.sum(lf[0] * lf[1])) - jnp.exp(jnp.sum(lf[2] * lf[3])) + lambda_init
    nb = T // ATTN_BLOCK
    qb = q.reshape(B, nb, ATTN_BLOCK, DIFF_HEADS, 2, DIFF_HEAD_DIM).transpose(1, 0, 2, 3, 4, 5)
    k_pos = jnp.arange(T)
    scale = DIFF_HEAD_DIM ** -0.5

    def block(args):
        qi, i = args
        s = jnp.einsum('bqhmd,bkhmd->bhmqk', qi, k).astype(jnp.float32) * scale
        q_pos = i * ATTN_BLOCK + jnp.arange(ATTN_BLOCK)
        mask = k_pos[None, :] <= q_pos[:, None]
        s = jnp.where(mask, s, -jnp.inf)
        p = jax.nn.softmax(s, axis=-1)
        a = p[:, :, 0] - lam * p[:, :, 1]
        return jnp.einsum('bhqk,bkhe->bqhe', a.astype(v.dtype), v)

    o = lax.map(block, (qb, jnp.arange(nb)))
    o = o.transpose(1, 0, 2, 3, 4).reshape(B, T, DIFF_HEADS, 2 * DIFF_HEAD_DIM)
    o = rmsnorm(o, norm_g) * (1.0 - lambda_init)
    return o.reshape(B, T, DIFF_WIDTH)


def gla_mixer(q, k, v, r, gate_code, gate_w2, gate_b, norm_g):
    B, T, _ = q.shape
    z = gate_code @ gate_w2 + gate_b
    g = jax.nn.log_sigmoid(z.astype(jnp.float32)) / GLA_TAU
    q = q.reshape(B, T, GLA_HEADS, GLA_DK) * (GLA_DK ** -0.5)
    k = k.reshape(B, T, GLA_HEADS, GLA_DK)
    vv = v.reshape(B, T, GLA_HEADS, GLA_DV)
    g = g.reshape(B, T, GLA_HEADS, GLA_DK)
    n = T // GLA_CHUNK

    def to_chunks(t):
        return t.reshape(B, n, GLA_CHUNK, GLA_HEADS, -1).transpose(1, 0, 3, 2, 4)

    qc, kc, vc, gc = to_chunks(q), to_chunks(k), to_chunks(vv), to_chunks(g)
    causal = jnp.tril(jnp.ones((GLA_CHUNK, GLA_CHUNK), dtype=bool))

    def step(S, inp):
        qi, ki, vi, gi = inp
        qf, kf, vf = qi.astype(jnp.float32), ki.astype(jnp.float32), vi.astype(jnp.float32)
        b = jnp.cumsum(gi, axis=2)
        o_inter = jnp.einsum('bhcd,bhde->bhce', qf * jnp.exp(b), S)
        diff = b[:, :, :, None, :] - b[:, :, None, :, :]
        decay = jnp.exp(jnp.where(causal[:, :, None], diff, -jnp.inf))
        A = jnp.einsum('bhid,bhjd,bhijd->bhij', qf, kf, decay)
        o_intra = jnp.einsum('bhij,bhje->bhie', A, vf)
        b_last = b[:, :, -1:, :]
        S = jnp.exp(b_last[:, :, 0, :])[..., None] * S + \
            jnp.einsum('bhcd,bhce->bhde', kf * jnp.exp(b_last - b), vf)
        return S, o_inter + o_intra

    S0 = jnp.zeros((B, GLA_HEADS, GLA_DK, GLA_DV), jnp.float32)
    _, o = lax.scan(step, S0, (qc, kc, vc, gc))
    o = o.transpose(1, 0, 3, 2, 4).reshape(B, T, GLA_HEADS, GLA_DV)
    o = rmsnorm(o, norm_g).astype(v.dtype).reshape(B, T, GLA_WIDTH)
    return o * jax.nn.silu(r)


def swiglu(h, w_in, w_out):
    gate, up = jnp.split(h @ w_in, 2, axis=-1)
    return (jax.nn.silu(gate) * up) @ w_out


def setup_inputs(seed: int = 0) -> dict:
    key = jax.random.key(seed)
    ks = jax.random.split(key, 16)
    f32 = jnp.float32
    nrm = lambda k, shape, s: jax.random.normal(k, shape, f32) * s
    return {
        "x": nrm(ks[0], (BATCH, SEQ, D_MODEL), 1.0),
        "norm1_g": 1.0 + nrm(ks[1], (DEPTH, D_MODEL), 0.02),
        "w_in": nrm(ks[2], (DEPTH, D_MODEL, IN_WIDTH), D_MODEL ** -0.5),
        "gmlp_ln_g": 1.0 + nrm(ks[3], (DEPTH, GMLP_WIDTH), 0.02),
        "spatial_w": nrm(ks[4], (DEPTH, GMLP_GROUPS, GMLP_CHUNK, GMLP_CHUNK), GMLP_CHUNK ** -0.5),
        "spatial_b": 1.0 + nrm(ks[5], (DEPTH, GMLP_GROUPS, GMLP_CHUNK), 0.1),
        "diff_lambdas": nrm(ks[6], (DEPTH, 4, DIFF_HEAD_DIM), 0.1),
        "diff_norm_g": 1.0 + nrm(ks[7], (DEPTH, 2 * DIFF_HEAD_DIM), 0.02),
        "gla_gate_w2": nrm(ks[8], (DEPTH, GLA_GATE_RANK, GLA_KEY_WIDTH), GLA_GATE_RANK ** -0.5),
        "gla_gate_b": nrm(ks[9], (DEPTH, GLA_KEY_WIDTH), 0.1),
        "gla_norm_g": 1.0 + nrm(ks[10], (DEPTH, GLA_DV), 0.02),
        "w_out": nrm(ks[11], (DEPTH, MIX_WIDTH, D_MODEL), MIX_WIDTH ** -0.5),
        "norm2_g": 1.0 + nrm(ks[12], (DEPTH, D_MODEL), 0.02),
        "w_ffn_in": nrm(ks[13], (DEPTH, D_MODEL, 2 * FFN_HIDDEN), D_MODEL ** -0.5),
        "w_ffn_out": nrm(ks[14], (DEPTH, FFN_HIDDEN, D_MODEL), FFN_HIDDEN ** -0.5),
        "final_g": 1.0 + nrm(ks[15], (D_MODEL,), 0.02),
    }


def reference(x, norm1_g, w_in, gmlp_ln_g, spatial_w, spatial_b, diff_lambdas, diff_norm_g,
              gla_gate_w2, gla_gate_b, gla_norm_g, w_out, norm2_g, w_ffn_in, w_ffn_out, final_g):
    for l in range(DEPTH):
        lambda_init = 0.8 - 0.6 * math.exp(-0.3 * l)
        h = rmsnorm(x, norm1_g[l])
        (u, v, dq, dk, dv, gq, gk, gv, gr, gcode) = jnp.split(h @ w_in[l], IN_SPLIT_IDX, axis=-1)
        a_out = gmlp_mixer(u, v, gmlp_ln_g[l], spatial_w[l], spatial_b[l])
        b_out = diff_attention(dq, dk, dv, diff_lambdas[l], diff_norm_g[l], lambda_init)
        c_out = gla_mixer(gq, gk, gv, gr, gcode, gla_gate_w2[l], gla_gate_b[l], gla_norm_g[l])
        mix = jnp.concatenate([a_out, b_out, c_out], axis=-1) @ w_out[l]
        x = x + mix.astype(x.dtype)
        x = x + swiglu(rmsnorm(x, norm2_g[l]), w_ffn_in[l], w_ffn_out[l]).astype(x.dtype)
    return rmsnorm(x, final_g)
```

```python
import math
from contextlib import ExitStack

import numpy as np
import ml_dtypes

import concourse.bass as bass
import concourse.mybir as mybir
from concourse.bass_utils import run_bass_kernel_spmd

F32 = mybir.dt.float32
BF16 = mybir.dt.bfloat16
AF = mybir.ActivationFunctionType
ALU = mybir.AluOpType
AX = mybir.AxisListType

NCORES = 8
D = 2048
BLK = 512
NKC = D // 128
FFN = 5632
NHC = FFN // 128
EPS = 1e-6
CB_U, CB_V, CB_DQ, CB_DK, CB_DV, CB_GQ, CB_GK, CB_GV, CB_GR, CB_GC = 0, 2, 4, 7, 10, 13, 15, 17, 20, 23
N_WIN = 24
N_WOUT = 8
N_FIN = 44
N_FOUT = 24
OFF_WIN = 0
OFF_WOUT = OFF_WIN + N_WIN
OFF_FIN = OFF_WOUT + N_WOUT
OFF_FOUT = OFF_FIN + N_FIN
N_CHUNK = 104
CPR = N_CHUNK // NCORES
CH = 4096


class Sem:
    __slots__ = ("h", "cnt")

    def __init__(self, h):
        self.h = h
        self.cnt = 0


class Tok:
    __slots__ = ("sem", "val")

    def __init__(self, sem, val):
        self.sem = sem
        self.val = val


class Buf:
    def __init__(self, K, t, name):
        self.K = K
        self.t = t
        self.name = name
        self.w = {}
        self.r = {}
        self.dsem = None
        self.last_dma = None
        self.scoped = K.scope is not None
        K.allbufs.append(self)

    def __getitem__(self, key):
        return self.t[key]

    def sem(self):
        if self.dsem is None or self.dsem.cnt > 30000:
            sc = self.K.scope
            if not self.scoped:
                self.K.scope = None
            self.dsem = self.K.new_sem("d_" + self.name)
            self.K.scope = sc
        return self.dsem

    def add_read(self, tok):
        o = self.r.get(tok.sem)
        if o is None or o < tok.val:
            self.r[tok.sem] = tok.val

    def set_write(self, tok, acc=False):
        if acc:
            o = self.w.get(tok.sem)
            if o is None or o < tok.val:
                self.w[tok.sem] = tok.val
        else:
            self.w = {tok.sem: tok.val}
        self.r = {}

    def link_from(self, others):
        for o in others:
            for s_, v in list(o.w.items()) + list(o.r.items()):
                c = self.r.get(s_)
                if c is None or c < v:
                    self.r[s_] = v


class Eng:
    def __init__(self, K, name, e, inorder=False):
        self.K = K
        self.name = name
        self.e = e
        self.inorder = inorder
        self.sem = None
        self.seen = {}
        self.nsem = 0

    def new_epoch(self):
        sc = self.K.scope
        self.K.scope = None
        self.sem = self.K.new_sem("e_%s_%d" % (self.name, self.nsem))
        self.K.scope = sc
        self.nsem += 1

    def wait(self, tok):
        if tok is None:
            return
        if self.inorder and tok.sem is self.sem:
            return
        if self.seen.get(tok.sem, 0) >= tok.val:
            return
        self.e.wait_ge(tok.sem.h, tok.val)
        self.seen[tok.sem] = tok.val

    def wait_buf_r(self, b):
        for s, v in list(b.w.items()):
            self.wait(Tok(s, v))

    def wait_buf_w(self, b, skip_w=False):
        if not skip_w:
            for s, v in list(b.w.items()):
                self.wait(Tok(s, v))
        for s, v in list(b.r.items()):
            self.wait(Tok(s, v))


class Kern:
    def __init__(self, nc, es):
        self.nc = nc
        self.es = es
        self.nsem = 0
        self.allbufs = []
        self.free_sems = []
        self.scope_sems = []
        self.scope = None
        self.pe = Eng(self, "pe", nc.tensor, inorder=True)
        self.act = Eng(self, "act", nc.scalar)
        self.dve = Eng(self, "dve", nc.vector)
        self.pool = Eng(self, "pool", nc.gpsimd)
        self.sp = Eng(self, "sp", nc.sync)
        self.engs = [self.pe, self.act, self.dve, self.pool, self.sp]
        for e in self.engs:
            e.new_epoch()
        self.ccsem = self.new_sem("cc")
        self.uid = 0
        self.scope = None

    def new_sem(self, name):
        if self.free_sems:
            sm = self.free_sems.pop()
        else:
            self.nsem += 1
            sm = Sem(self.es.enter_context(self.nc.semaphore("s%d" % self.nsem)))
        if self.scope is not None:
            self.scope_sems.append(sm)
        return sm

    def end_scope(self):
        for sm in self.scope_sems:
            if sm.cnt < 20000:
                self.free_sems.append(sm)
        self.scope_sems = []
        self.allbufs = [b for b in self.allbufs if not b.scoped]

    def epoch(self):
        for e in (self.pe, self.act, self.dve, self.pool):
            e.new_epoch()

    def sb(self, shape, dt, name):
        self.uid += 1
        es = self.scope if self.scope is not None else self.es
        t = es.enter_context(self.nc.sbuf_tensor("%s_%d" % (name, self.uid), list(shape), dt))
        return Buf(self, t, name)

    def barrier(self):
        toks = []
        for e in (self.pe, self.act, self.dve, self.pool):
            if e.sem.cnt:
                toks.append(Tok(e.sem, e.sem.cnt))
        for b in self.allbufs:
            if b.last_dma is not None:
                toks.append(b.last_dma)
        if self.ccsem.cnt:
            toks.append(Tok(self.ccsem, self.ccsem.cnt))
        for e in self.engs:
            for t in toks:
                e.wait(t)

    def ps(self, shape, dt, name):
        self.uid += 1
        t = self.es.enter_context(self.nc.psum_tensor("%s_%d" % (name, self.uid), list(shape), dt))
        return Buf(self, t, name)

    def dram(self, shape, dt, name, kind=None):
        if kind is None:
            t = self.nc.dram_tensor(name, list(shape), dt)
        else:
            t = self.nc.dram_tensor(name, list(shape), dt, kind=kind)
        return Buf(self, t.ap(), name)

    def op(self, eng, fn, reads=(), writes=(), mark=True):
        for b in reads:
            eng.wait_buf_r(b)
        for b in writes:
            eng.wait_buf_w(b)
        ins = fn()
        if mark:
            eng.sem.cnt += 1
            ins.then_inc(eng.sem.h, 1)
            tok = Tok(eng.sem, eng.sem.cnt)
        else:
            tok = Tok(eng.sem, eng.sem.cnt + 1)
        for b in reads:
            b.add_read(tok)
        for b in writes:
            b.set_write(tok)
        return tok

    def dma(self, q, out_b, out_ap, in_b, in_ap, sem_b=None, serialize=True, acc=False):
        sb_ = sem_b if sem_b is not None else out_b
        S = sb_.sem()
        q.wait_buf_r(in_b)
        q.wait_buf_w(out_b, skip_w=acc)
        if serialize and sb_.last_dma is not None:
            q.wait(sb_.last_dma)
        ins = q.e.dma_start(out=out_ap, in_=in_ap)
        S.cnt += 16
        ins.then_inc(S.h, 16)
        tok = Tok(S, S.cnt)
        sb_.last_dma = tok
        in_b.add_read(tok)
        if acc:
            r_keep = out_b.r
            out_b.set_write(tok, acc=True)
            out_b.r = r_keep
        else:
            out_b.set_write(tok)
        return tok

    def allgather(self, in_b, in_ap, out_b, out_ap):
        q = self.pool
        q.wait_buf_r(in_b)
        q.wait_buf_w(out_b)
        ins = q.e.collective_compute(
            "AllGather", ALU.bypass, replica_groups=[list(range(NCORES))], ins=[in_ap], outs=[out_ap]
        )
        self.ccsem.cnt += 1
        ins.then_inc(self.ccsem.h, 1)
        tok = Tok(self.ccsem, self.ccsem.cnt)
        in_b.add_read(tok)
        out_b.set_write(tok)
        return tok

    def mm(self, out_b, out_ap, l_b, l_ap, r_b, r_ap, start, stop, **kw):
        return self.op(
            self.pe,
            lambda: self.nc.tensor.matmul(out_ap, l_ap, r_ap, start=start, stop=stop, **kw),
            reads=(l_b, r_b),
            writes=(out_b,),
            mark=stop,
        )

    def tr(self, out_b, out_ap, in_b, in_ap, id_b, id_ap, mark=True):
        return self.op(
            self.pe,
            lambda: self.nc.tensor.transpose(out_ap, in_ap, id_ap),
            reads=(in_b, id_b),
            writes=(out_b,),
            mark=mark,
        )

    def actf(self, out_b, out_ap, in_b, in_ap, func, extra_reads=(), **kw):
        return self.op(
            self.act,
            lambda: self.nc.scalar.activation(out=out_ap, in_=in_ap, func=func, **kw),
            reads=(in_b,) + tuple(extra_reads),
            writes=(out_b,),
        )

    def v(self, eng, name, out_b, reads, *args, **kw):
        return self.op(eng, lambda: getattr(eng.e, name)(*args, **kw), reads=reads, writes=(out_b,))


class Ring:
    def __init__(self, K, nslot):
        self.K = K
        self.slots = [K.sb([128, CH], BF16, "ring%d" % i) for i in range(nslot)]
        self.n = nslot
        self.issued = 0
        self.taken = 0
        self.plan = []
        self.pos = 0

    def set_plan(self, plan):
        self.plan = plan
        self.pos = 0
        self.issued = 0
        self.taken = 0

    def _issue(self):
        db, dap = self.plan[self.issued]
        s = self.slots[self.issued % self.n]
        self.K.dma(self.K.sp, s, s[:, :], db, dap)
        self.issued += 1

    def take(self):
        self.prefetch()
        assert self.issued > self.taken
        s = self.slots[self.taken % self.n]
        self.taken += 1
        return s

    def prefetch(self):
        while self.issued < len(self.plan) and self.issued < self.taken + self.n - 2:
            self._issue()


def lambda_init(l):
    return 0.8 - 0.6 * math.exp(-0.3 * l)


class Prog:
    def __init__(self, depth, nslot, debug=False):
        self.depth = depth
        self.nslot = nslot
        self.debug = debug
        self.ntok = nslot * BLK
        self.nc = bass.Bass("TRN2", target_bir_lowering=False)
        self.es = ExitStack()

    def declare(self):
        K = self.K
        nc = self.nc
        L, J = self.depth, self.nslot
        ext = lambda name, shape, dt=F32: K.dram(shape, dt, name, kind="ExternalInput")
        self.x_in = ext("x", [J, BLK, D])
        self.wsh = ext("wsh", [L, CPR, 128, CH])
        self.g1col = ext("g1col", [L, 128, NKC])
        self.g2col = ext("g2col", [L, 128, NKC])
        self.lngcol = ext("lngcol", [L, 128, 4])
        self.swT = ext("swT", [L, 128, 4, 128])
        self.sbrow = ext("sbrow", [L, 1, 512])
        self.lamrow = ext("lamrow", [L, 1, 256])
        self.dngcol = ext("dngcol", [L, 128, 1])
        self.w2 = ext("w2", [L, 16, 384])
        self.gbrow = ext("gbrow", [L, 1, 384])
        self.gngcol = ext("gngcol", [L, 128, 6])
        self.fgrow = ext("fgrow", [1, D])
        self.c_ident = ext("c_ident", [128, 128], BF16)
        self.c_identf = ext("c_identf", [128, 128])
        self.c_uneg = ext("c_uneg", [128, 128])
        self.c_lneg = ext("c_lneg", [128, 128])
        self.c_masku = ext("c_masku", [128, 128])
        self.c_diag = ext("c_diag", [128, 4, 512], BF16)
        self.c_flag = ext("c_flag", [128, 8])
        self.c_sel = ext("c_sel", [128, 8])
        okind = "ExternalOutput"
        self.out = K.dram([J, BLK, D], F32, "out", kind=okind)
        dbg = okind if self.debug else None
        self.xres = K.dram([J, BLK, D], F32, "xres")
        self.wloc = [K.dram([CPR * 128, CH], BF16, "wloc%d" % l) for l in range(L)]
        self.wall = [K.dram([N_CHUNK * 128, CH], BF16, "wall%d" % l) for l in range(L)]
        for b in self.wloc:
            b.sem()
        self.kT_loc = K.dram([J * 6 * 128, BLK], BF16, "kT_loc", kind=dbg)
        self.kT_all = K.dram([NCORES * J * 6 * 128, BLK], BF16, "kT_all")
        self.v_loc = K.dram([J * 6 * BLK, 128], BF16, "v_loc", kind=dbg)
        self.v_all = K.dram([NCORES * J * 6 * BLK, 128], BF16, "v_all")
        self.st_loc = K.dram([J * 4 * 96, 193], F32, "st_loc", kind=dbg)
        self.st_all = K.dram([NCORES * J * 4 * 96, 193], F32, "st_all")
        self.qT_s = K.dram([J, 6, 128, BLK], BF16, "qT_s", kind=dbg)
        self.aT_s = K.dram([J, 4, 128, BLK], BF16, "aT_s", kind=dbg)
        self.ol_s = K.dram([J, BLK, 768], F32, "ol_s", kind=dbg)
        self.qf_s = K.dram([J, 4, 96, BLK], BF16, "qf_s", kind=dbg)
        self.sr_s = K.dram([J, BLK, 768], BF16, "sr_s", kind=dbg)

    def wchunk(self, l, idx):
        r, i = idx // CPR, idx % CPR
        row = (r * CPR + i) * 128
        return (self.wall[l], self.wall[l][row:row + 128, :])

    def load_consts(self):
        K = self.K
        q = K.pool

        def ld(src, shape, dt, name):
            b = K.sb(shape, dt, name)
            K.dma(q, b, b[:], src, src[:])
            return b

        self.ident = ld(self.c_ident, [128, 128], BF16, "ident")
        self.identf = ld(self.c_identf, [128, 128], F32, "identf")
        self.uneg = ld(self.c_uneg, [128, 128], F32, "uneg")
        self.lneg = ld(self.c_lneg, [128, 128], F32, "lneg")
        self.masku = ld(self.c_masku, [128, 128], F32, "masku")
        self.diag = ld(self.c_diag, [128, 4, 512], BF16, "diag")
        self.flag = ld(self.c_flag, [128, 8], F32, "flag")
        self.sel = ld(self.c_sel, [128, 8], F32, "sel")
        self.ones_f = K.sb([128, 512], F32, "ones_f")
        K.op(K.dve, lambda: self.nc.vector.memset(self.ones_f[:], 1.0), writes=(self.ones_f,))
        self.ones_b = K.sb([128, 128], BF16, "ones_b")
        K.op(K.dve, lambda: self.nc.vector.memset(self.ones_b[:], 1.0), writes=(self.ones_b,))
        self.flag_b = K.sb([128, 8], BF16, "flag_b")
        K.op(K.dve, lambda: self.nc.vector.tensor_copy(out=self.flag_b[:], in_=self.flag[:]),
             reads=(self.flag,), writes=(self.flag_b,))

    def prep_layer(self, l):
        K = self.K
        src = self.wsh[l].rearrange("c p f -> (c p) f")
        for i in range(CPR):
            K.dma(K.pool, self.wloc[l], self.wloc[l][i * 128:(i + 1) * 128, :], self.wsh,
                  src[i * 128:(i + 1) * 128, :], serialize=False, acc=True)
        K.allgather(self.wloc[l], self.wloc[l][:, :], self.wall[l], self.wall[l][:, :])

    def prep_weights(self):
        self.prep_layer(0)

    def setup_psum(self):
        K = self.K
        self.banks = [K.ps([128, 512], F32, "bank%d" % i) for i in range(8)]
        self.accb = self.banks[4:8]
        self.bi = 0
        self.si = 0

    def bank(self):
        b = self.banks[self.bi % 8]
        self.bi += 1
        return b

    def sbank(self):
        b = self.banks[self.si % 4]
        self.si += 1
        return b

    def bbank(self):
        b = self.bank()
        return b, b.t[:, :].bitcast(BF16)

    def bcast_row(self, dst, dst_ap, row_b, row_ap, n):
        K = self.K
        bk = self.bank()
        K.mm(bk, bk[:, 0:n], self.ones_f, self.ones_f[0:1, 0:128], row_b, row_ap, True, True)
        K.op(K.dve, lambda: self.nc.vector.tensor_copy(out=dst_ap, in_=bk[:, 0:n]), reads=(bk,), writes=(dst,))

    def gelu(self, out_b, out_ap, in_b, in_ap, tmp, tmp_ap, xs, xs_ap):
        K, nc = self.K, self.nc
        K.actf(xs, xs_ap, in_b, in_ap, AF.Copy)
        K.op(K.dve, lambda: nc.vector.tensor_tensor(out=tmp_ap, in0=xs_ap, in1=xs_ap, op=ALU.mult),
             reads=(xs,), writes=(tmp,))
        K.op(K.dve, lambda: nc.vector.tensor_scalar(out=tmp_ap, in0=tmp_ap, scalar1=0.044715, scalar2=1.0,
                                                   op0=ALU.mult, op1=ALU.add), reads=(tmp,), writes=(tmp,))
        K.op(K.dve, lambda: nc.vector.tensor_tensor(out=tmp_ap, in0=tmp_ap, in1=xs_ap, op=ALU.mult),
             reads=(tmp, xs), writes=(tmp,))
        K.actf(tmp, tmp_ap, tmp, tmp_ap, AF.Sigmoid, scale=1.5957691216057308)
        K.op(K.dve, lambda: nc.vector.tensor_tensor(out=out_ap, in0=tmp_ap, in1=xs_ap, op=ALU.mult),
             reads=(tmp, xs), writes=(out_b,))

    def rms_rstd(self, rstd, rstd_ap, ssq, ssq_ap, n):
        K, nc = self.K, self.nc
        K.actf(rstd, rstd_ap, ssq, ssq_ap, AF.Sqrt, scale=1.0 / n, bias=EPS)
        K.op(K.dve, lambda: nc.vector.reciprocal(out=rstd_ap, in_=rstd_ap), reads=(rstd,), writes=(rstd,))

    def norm_to_hT(self, xsrc_b, xsrc_aps, gcol, hT, keep=None):
        K, nc = self.K, self.nc
        for tt in range(4):
            x = keep[tt] if keep is not None else self.xt[tt % 2]
            K.dma(K.sp, x, x[:, :], xsrc_b, xsrc_aps[tt])
            K.op(K.act, lambda x=x, tt=tt: nc.scalar.activation(out=self.xn[:, tt, :], in_=x[:, :], func=AF.Square,
                                                             accum_out=self.ssq[:, tt:tt + 1]),
                 reads=(x,), writes=(self.xn, self.ssq))
            self.rms_rstd(self.rstd, self.rstd[:, tt:tt + 1], self.ssq, self.ssq[:, tt:tt + 1], D)
            K.op(K.dve, lambda x=x, tt=tt: nc.vector.tensor_scalar(out=self.xn[:, tt, :], in0=x[:, :],
                                                                  scalar1=self.rstd[:, tt:tt + 1], scalar2=None,
                                                                  op0=ALU.mult),
                 reads=(x, self.rstd), writes=(self.xn,))
        for kc in range(NKC):
            bk, bv = self.bbank()
            for tt in range(4):
                K.tr(bk, bv[:, tt * 128:(tt + 1) * 128], self.xn, self.xn[:, tt, kc * 128:(kc + 1) * 128],
                     self.ident, self.ident[:, :], mark=(tt == 3))
            K.op(K.act, lambda kc=kc, bk=bk, bv=bv: nc.scalar.activation(out=hT[:, kc, :], in_=bv[:, 0:512],
                                                                        func=AF.Copy, scale=gcol[:, kc:kc + 1]),
                 reads=(bk, gcol), writes=(hT,))

    def load_layer_params(self, l):
        K, nc = self.K, self.nc
        q = K.sp
        P = {}

        def ld(src_b, src_ap, shape, dt, name):
            b = self.lp[name]
            K.dma(q, b, b[:], src_b, src_ap)
            return b

        P["g1"] = ld(self.g1col, self.g1col[l], None, None, "g1")
        P["g2"] = ld(self.g2col, self.g2col[l], None, None, "g2")
        P["lng"] = ld(self.lngcol, self.lngcol[l], None, None, "lng")
        P["swTf"] = ld(self.swT, self.swT[l], None, None, "swTf")
        P["sbrow"] = ld(self.sbrow, self.sbrow[l], None, None, "sbrow")
        P["lam"] = ld(self.lamrow, self.lamrow[l], None, None, "lam")
        P["dng"] = ld(self.dngcol, self.dngcol[l], None, None, "dng")
        P["w2"] = ld(self.w2, self.w2[l], None, None, "w2")
        P["gb"] = ld(self.gbrow, self.gbrow[l], None, None, "gb")
        P["gng"] = ld(self.gngcol, self.gngcol[l], None, None, "gng")
        swb = self.lp["swTb"]
        for g in range(4):
            K.op(K.dve, lambda g=g: nc.vector.tensor_tensor(out=swb[:, g, :], in0=P["swTf"][:, g, :],
                                                           in1=self.masku[:, :], op=ALU.mult),
                 reads=(P["swTf"], self.masku), writes=(swb,))
        P["swTb"] = swb
        sbb = self.lp["sbb"]
        tmp = self.lp["sbtmp"]
        self.bcast_row(tmp, tmp[:, :], P["sbrow"], P["sbrow"][0:1, :], 512)
        for g in range(4):
            for tt in range(4):
                K.op(K.pool, lambda g=g, tt=tt: nc.gpsimd.tensor_copy(out=sbb[:, g, tt, :],
                                                                     in_=tmp[:, g * 128:(g + 1) * 128]),
                     reads=(tmp,), writes=(sbb,))
        P["sbb"] = sbb
        lw = self.lp["lamw"]
        lam = P["lam"]
        K.op(K.dve, lambda: nc.vector.tensor_tensor(out=lw[0:1, 0:64], in0=lam[0:1, 0:64], in1=lam[0:1, 64:128],
                                                   op=ALU.mult), reads=(lam,), writes=(lw,))
        K.op(K.dve, lambda: nc.vector.tensor_tensor(out=lw[0:1, 64:128], in0=lam[0:1, 128:192],
                                                   in1=lam[0:1, 192:256], op=ALU.mult), reads=(lam,), writes=(lw,))
        K.op(K.dve, lambda: nc.vector.tensor_reduce(out=lw[0:1, 128:130],
                                                   in_=lw[0:1, 0:128].rearrange("p (a b) -> p a b", a=2),
                                                   axis=AX.X, op=ALU.add), reads=(lw,), writes=(lw,))
        K.actf(lw, lw[0:1, 130:132], lw, lw[0:1, 128:130], AF.Exp)
        K.op(K.dve, lambda: nc.vector.scalar_tensor_tensor(out=lw[0:1, 132:133], in0=lw[0:1, 131:132],
                                                          scalar=-lambda_init(l), in1=lw[0:1, 130:131],
                                                          op0=ALU.add, op1=ALU.subtract), reads=(lw,), writes=(lw,))
        nl = self.lp["nlam"]
        self.bcast_row(nl, nl[:, 0:1], lw, lw[0:1, 132:133], 1)
        P["nlam"] = nl
        dg = self.lp["dngs"]
        K.op(K.dve, lambda: nc.vector.tensor_scalar(out=dg[:, :], in0=P["dng"][:, :], scalar1=1.0 - lambda_init(l),
                                                   scalar2=None, op0=ALU.mult), reads=(P["dng"],), writes=(dg,))
        P["dngs"] = dg
        return P

    def alloc_layer_params(self):
        K = self.K
        sh = {
            "g1": ([128, NKC], F32), "g2": ([128, NKC], F32), "lng": ([128, 4], F32),
            "swTf": ([128, 4, 128], F32), "swTb": ([128, 4, 128], BF16), "sbrow": ([1, 512], F32),
            "sbtmp": ([128, 512], F32), "sbb": ([128, 4, 4, 128], F32), "lam": ([1, 256], F32),
            "lamw": ([1, 136], F32), "nlam": ([128, 1], F32), "dng": ([128, 1], F32), "dngs": ([128, 1], F32),
            "w2": ([16, 384], F32), "gb": ([1, 384], F32), "gng": ([128, 6], F32),
        }
        self.lp = {k: K.sb(s, dt, "lp_" + k) for k, (s, dt) in sh.items()}

    def phase1(self, l, P):
        K, nc = self.K, self.nc
        J = self.nslot
        ring = self.ring
        plan = []
        for j in range(J):
            for cb in range(N_WIN):
                plan.append(self.wchunk(l, OFF_WIN + cb))
        ring.set_plan(plan)
        xsrc = self.x_in if l == 0 else self.xres
        for j in range(J):
            ring.prefetch()
            self.norm_to_hT(xsrc, [xsrc[j, tt * 128:(tt + 1) * 128, :] for tt in range(4)], P["g1"], self.hT)
            hT = self.hT

            def wv(slot):
                return slot[:, :].rearrange("p (k c) -> p k c", k=16)

            def fm(slot, c0, M, bk, ncols=512):
                w = wv(slot)
                for kc in range(NKC):
                    K.mm(bk, bk[0:M, 0:ncols], slot, w[:, kc, c0:c0 + M], hT, hT[:, kc, 0:ncols], kc == 0, kc == NKC - 1)

            def tm(slot, c0, N, tt, bk, o0=0):
                w = wv(slot)
                for kc in range(NKC):
                    K.mm(bk, bk[:, o0:o0 + N], hT, hT[:, kc, tt * 128:(tt + 1) * 128], slot, w[:, kc, c0:c0 + N],
                         kc == 0, kc == NKC - 1)

            for cb in range(2):
                s = ring.take()
                for hh in range(2):
                    g = cb * 2 + hh
                    bk = self.bank()
                    fm(s, hh * 128, 128, bk)
                    self.gelu(self.uT, self.uT[:, g, :], bk, bk[:, :], self.t1, self.t1[:, 0:512], self.t2,
                              self.t2[:, 0:512])
            s0 = ring.take()
            s1 = ring.take()
            for tt in range(4):
                bk = self.bank()
                tm(s0, 0, 256, tt, bk, 0)
                tm(s1, 0, 256, tt, bk, 256)
                self.gelu(self.t3, self.t3[:, 0:512], bk, bk[:, :], self.t1, self.t1[:, 0:512], self.t2,
                          self.t2[:, 0:512])
                K.op(K.dve, lambda: nc.vector.bn_stats(out=self.bst[:, :], in_=self.t3[:, 0:512]),
                     reads=(self.t3,), writes=(self.bst,))
                K.op(K.dve, lambda: nc.vector.bn_aggr(out=self.bag[:, :], in_=self.bst[:, :]),
                     reads=(self.bst,), writes=(self.bag,))
                K.actf(self.bag, self.bag[:, 1:2], self.bag, self.bag[:, 1:2], AF.Sqrt, bias=EPS)
                K.op(K.dve, lambda: nc.vector.reciprocal(out=self.bag[:, 1:2], in_=self.bag[:, 1:2]),
                     reads=(self.bag,), writes=(self.bag,))
                K.op(K.dve, lambda tt=tt: nc.vector.tensor_scalar(out=self.vh[:, tt, :], in0=self.t3[:, 0:512],
                                                                 scalar1=self.bag[:, 0:1], scalar2=self.bag[:, 1:2],
                                                                 op0=ALU.subtract, op1=ALU.mult),
                     reads=(self.t3, self.bag), writes=(self.vh,))
            for g in range(4):
                bk = self.bank()
                for tt in range(4):
                    K.mm(bk, bk[:, tt * 128:(tt + 1) * 128], self.vh, self.vh[:, tt, g * 128:(g + 1) * 128],
                         P["swTb"], P["swTb"][:, g, :], True, True)
                K.op(K.dve, lambda g=g, bk=bk: nc.vector.scalar_tensor_tensor(
                    out=self.t1[:, 0:512], in0=bk[:, :], scalar=P["lng"][:, g:g + 1],
                    in1=P["sbb"][:, g, :, :].rearrange("p a b -> p (a b)"), op0=ALU.mult, op1=ALU.add),
                     reads=(bk, P["lng"], P["sbb"]), writes=(self.t1,))
                st = self.stg[self.stgi % 2]
                self.stgi += 1
                K.op(K.dve, lambda g=g, st=st: nc.vector.tensor_tensor(out=st[:, 0:512], in0=self.t1[:, 0:512],
                                                                      in1=self.uT[:, g, :], op=ALU.mult),
                     reads=(self.t1, self.uT), writes=(st,))
                K.dma(K.sp, self.aT_s, self.aT_s[j, g], st, st[:, 0:512], sem_b=st, acc=True)
            for which in range(2):
                for cb in range(3):
                    s = ring.take()
                    for hh in range(2):
                        h = cb * 2 + hh
                        bk = self.bank()
                        fm(s, hh * 128, 128, bk)
                        st = self.stg[self.stgi % 2]
                        self.stgi += 1
                        if which == 0:
                            K.actf(st, st[:, 0:512], bk, bk[:, :], AF.Copy, scale=0.125)
                            K.dma(K.sp, self.qT_s, self.qT_s[j, h], st, st[:, 0:512], sem_b=st, acc=True)
                        else:
                            K.op(K.dve, lambda st=st, bk=bk: nc.vector.tensor_copy(out=st[:, 0:512], in_=bk[:, :]),
                                 reads=(bk,), writes=(st,))
                            r0 = (j * 6 + h) * 128
                            K.dma(K.sp, self.kT_loc, self.kT_loc[r0:r0 + 128, :], st, st[:, 0:512], sem_b=st, acc=True)
            ss = [ring.take() for _ in range(3)]
            for tt in range(4):
                st = self.stgv[self.stgvi % 2]
                self.stgvi += 1
                bk = self.bank()
                tm(ss[0], 0, 256, tt, bk, 0)
                tm(ss[1], 0, 256, tt, bk, 256)
                K.actf(st, st[:, 0:512], bk, bk[:, :], AF.Copy)
                bk = self.bank()
                tm(ss[2], 0, 256, tt, bk, 0)
                K.op(K.dve, lambda st=st, bk=bk: nc.vector.tensor_copy(out=st[:, 512:768], in_=bk[:, 0:256]),
                     reads=(bk,), writes=(st,))
                vdst = self.v_loc[j * 3072:(j + 1) * 3072, :].rearrange("(h t) e -> t h e", h=6)
                K.dma(K.sp, self.v_loc, vdst[tt * 128:(tt + 1) * 128, :, :], st,
                      st[:, 0:768].rearrange("p (h e) -> p h e", h=6), sem_b=st, acc=True)
            for cb in range(2):
                s = ring.take()
                for hh in range(2):
                    h = cb * 2 + hh
                    bk = self.bank()
                    fm(s, hh * 96, 96, bk)
                    K.actf(self.gqT, self.gqT[0:96, h, :], bk, bk[0:96, :], AF.Copy)
            for cb in range(2):
                s = ring.take()
                for hh in range(2):
                    h = cb * 2 + hh
                    bk = self.bank()
                    fm(s, hh * 96, 96, bk)
                    K.op(K.dve, lambda h=h, bk=bk: nc.vector.tensor_copy(out=self.gkT[0:96, h, :], in_=bk[0:96, :]),
                         reads=(bk,), writes=(self.gkT,))
                for tt in range(4):
                    bk = self.bank()
                    tm(s, 0, 192, tt, bk, 0)
                    K.actf(self.gk_tm, self.gk_tm[:, tt, cb * 192:(cb + 1) * 192], bk, bk[:, 0:192], AF.Copy)
            ss = [ring.take() for _ in range(3)]
            for tt in range(4):
                bk = self.bank()
                tm(ss[0], 0, 256, tt, bk, 0)
                tm(ss[1], 0, 256, tt, bk, 256)
                K.actf(self.gv_tm, self.gv_tm[:, tt, 0:512], bk, bk[:, :], AF.Copy)
                bk = self.bank()
                tm(ss[2], 0, 256, tt, bk, 0)
                K.op(K.dve, lambda tt=tt, bk=bk: nc.vector.tensor_copy(out=self.gv_tm[:, tt, 512:768],
                                                                      in_=bk[:, 0:256]),
                     reads=(bk,), writes=(self.gv_tm,))
            ss = [ring.take() for _ in range(3)]
            for tt in range(4):
                st = self.stgv[self.stgvi % 2]
                self.stgvi += 1
                bk = self.bank()
                tm(ss[0], 0, 256, tt, bk, 0)
                tm(ss[1], 0, 256, tt, bk, 256)
                K.actf(st, st[:, 0:512], bk, bk[:, :], AF.Silu)
                bk = self.bank()
                tm(ss[2], 0, 256, tt, bk, 0)
                K.actf(st, st[:, 512:768], bk, bk[:, 0:256], AF.Silu)
                K.dma(K.sp, self.sr_s, self.sr_s[j, tt * 128:(tt + 1) * 128, :], st, st[:, 0:768], sem_b=st, acc=True)
            s = ring.take()
            bk = self.bank()
            fm(s, 0, 16, bk)
            K.actf(self.gcT, self.gcT[0:16, :], bk, bk[0:16, :], AF.Copy)
            ring.prefetch()
            self.gla_local(l, j, P)

    def gla_local(self, l, j, P):
        K, nc = self.K, self.nc
        S, Sb, pc = self.S, self.Sb, self.pcum
        qscale = 96 ** -0.5
        for tt in range(4):
            tsl = slice(tt * 128, (tt + 1) * 128)
            bz = self.bank()
            K.mm(bz, bz[:, 0:384], self.gcT, self.gcT[0:16, tsl], P["w2"], P["w2"][0:16, :], True, False)
            K.mm(bz, bz[:, 0:384], self.ones_f, self.ones_f[0:1, 0:128], P["gb"], P["gb"][0:1, :], False, True)
            K.actf(self.sp_, self.sp_[:, :], bz, bz[:, 0:384], AF.Exp, scale=-1.0)
            K.actf(self.sp_, self.sp_[:, :], self.sp_, self.sp_[:, :], AF.Ln, bias=1.0)
            brb = self.bank()
            K.mm(brb, brb[:, 0:384], self.lneg, self.lneg[:, :], self.sp_, self.sp_[:, :], True, True)
            K.actf(self.erb, self.erb[:, :], brb, brb[:, 0:384], AF.Exp)
            K.op(K.dve, lambda tt=tt: nc.vector.tensor_tensor(out=self.khat[:, :], in0=self.gk_tm[:, tt, :],
                                                             in1=self.erb[:, :], op=ALU.mult),
                 reads=(self.gk_tm, self.erb), writes=(self.khat,))
            for h in range(4):
                hs = slice(h * 96, (h + 1) * 96)
                bb_ = self.bank()
                K.mm(bb_, bb_[0:96, 0:128], self.sp_, self.sp_[:, hs], self.uneg, self.uneg[:, :], True, True)
                K.actf(self.e1, self.e1[0:96, :], bb_, bb_[0:96, 0:128], AF.Exp)
                K.actf(self.e2, self.e2[0:96, :], bb_, bb_[0:96, 0:128], AF.Exp, scale=-1.0)
                K.op(K.pool, lambda h=h, tt=tt: nc.gpsimd.tensor_copy(out=self.ecol[0:96, tt, h:h + 1],
                                                                     in_=self.e1[0:96, 127:128]),
                     reads=(self.e1,), writes=(self.ecol,))
                K.op(K.dve, lambda h=h, tsl=tsl: nc.vector.scalar_tensor_tensor(
                    out=self.qtT[0:96, h, :], in0=self.gqT[0:96, h, tsl], scalar=qscale, in1=self.e1[0:96, :],
                    op0=ALU.mult, op1=ALU.mult), reads=(self.gqT, self.e1), writes=(self.qtT,))
                K.op(K.dve, lambda h=h, tsl=tsl: nc.vector.tensor_tensor(
                    out=self.ktT[0:96, h, :], in0=self.gkT[0:96, h, tsl], in1=self.e2[0:96, :], op=ALU.mult),
                     reads=(self.gkT, self.e2), writes=(self.ktT,))
                if tt == 0:
                    K.op(K.dve, lambda h=h, tsl=tsl: nc.vector.tensor_copy(out=self.qfst[0:96, h, tsl],
                                                                          in_=self.qtT[0:96, h, :]),
                         reads=(self.qtT,), writes=(self.qfst,))
                else:
                    K.op(K.dve, lambda h=h, tsl=tsl: nc.vector.tensor_scalar(
                        out=self.qfst[0:96, h, tsl], in0=self.qtT[0:96, h, :], scalar1=pc[0:96, h:h + 1],
                        scalar2=None, op0=ALU.mult), reads=(self.qtT, pc), writes=(self.qfst,))
                ba = self.bank()
                K.mm(ba, ba[:, 0:128], self.ktT, self.ktT[0:96, h, :], self.qtT, self.qtT[0:96, h, :], True, True)
                K.op(K.dve, lambda h=h, ba=ba: nc.vector.tensor_tensor(out=self.AT[:, h, :], in0=ba[:, 0:128],
                                                                      in1=self.masku[:, :], op=ALU.mult),
                     reads=(ba, self.masku), writes=(self.AT,))
            for half in range(2):
                bo = self.bank()
                for hh in range(2):
                    h = half * 2 + hh
                    vs = slice(h * 192, (h + 1) * 192)
                    K.mm(bo, bo[:, hh * 192:(hh + 1) * 192], self.AT, self.AT[:, h, :], self.gv_tm,
                         self.gv_tm[:, tt, vs], True, tt == 0)
                    if tt > 0:
                        K.mm(bo, bo[:, hh * 192:(hh + 1) * 192], self.qtT, self.qtT[0:96, h, :], Sb,
                             Sb[0:96, h, :], False, True)
                if half == 0:
                    st = self.olst[self.olsti % 2]
                    self.olsti += 1
                    K.actf(st, st[:, 0:384], bo, bo[:, 0:384], AF.Copy)
                else:
                    K.op(K.dve, lambda st=st, bo=bo: nc.vector.tensor_copy(out=st[:, 384:768], in_=bo[:, 0:384]),
                         reads=(bo,), writes=(st,))
            K.dma(K.sp, self.ol_s, self.ol_s[j, tsl, :], st, st[:, 0:768], sem_b=st, acc=True)
            for h in range(4):
                hs = slice(h * 96, (h + 1) * 96)
                vs = slice(h * 192, (h + 1) * 192)
                bs = self.bank()
                K.mm(bs, bs[0:96, 0:192], self.khat, self.khat[:, hs], self.gv_tm, self.gv_tm[:, tt, vs], True, True)
                if tt == 0:
                    K.op(K.dve, lambda h=h, bs=bs: nc.vector.tensor_copy(out=S[0:96, h, :], in_=bs[0:96, 0:192]),
                         reads=(bs,), writes=(S,))
                else:
                    K.op(K.dve, lambda h=h, bs=bs: nc.vector.scalar_tensor_tensor(
                        out=S[0:96, h, :], in0=S[0:96, h, :], scalar=self.ecol[0:96, tt, h:h + 1],
                        in1=bs[0:96, 0:192], op0=ALU.mult, op1=ALU.add), reads=(S, self.ecol, bs), writes=(S,))
                K.op(K.pool, lambda h=h: nc.gpsimd.tensor_copy(out=Sb[0:96, h, :], in_=S[0:96, h, :]),
                     reads=(S,), writes=(Sb,))
            if tt == 0:
                K.op(K.dve, lambda: nc.vector.tensor_copy(out=pc[0:96, 0:4], in_=self.ecol[0:96, 0, 0:4]),
                     reads=(self.ecol,), writes=(pc,))
            else:
                K.op(K.dve, lambda tt=tt: nc.vector.tensor_tensor(out=pc[0:96, 0:4], in0=pc[0:96, 0:4],
                                                                 in1=self.ecol[0:96, tt, 0:4], op=ALU.mult),
                     reads=(pc, self.ecol), writes=(pc,))
        K.dma(K.sp, self.qf_s, self.qf_s[j].rearrange("h d t -> d h t"), self.qfst, self.qfst[0:96, :, :],
              sem_b=self.qfst, acc=True)
        K.op(K.dve, lambda: nc.vector.tensor_copy(out=self.stst[0:96, :, 0:192], in_=S[0:96, :, :]),
             reads=(S,), writes=(self.stst,))
        K.op(K.dve, lambda: nc.vector.tensor_copy(out=self.stst[0:96, :, 192:193],
                                                 in_=pc[0:96, 0:4].rearrange("p (h o) -> p h o", o=1)),
             reads=(pc,), writes=(self.stst,))
        r0 = j * 4 * 96
        K.dma(K.sp, self.st_loc, self.st_loc[r0:r0 + 384, :].rearrange("(h d) e -> d h e", h=4), self.stst,
              self.stst[0:96, :, :], sem_b=self.stst, acc=True)

    def alloc_p1(self):
        K = self.K
        self.xt = [K.sb([128, D], F32, "xt%d" % i) for i in range(2)]
        self.ssq = K.sb([128, 4], F32, "ssq")
        self.rstd = K.sb([128, 4], F32, "rstd")
        self.xn = K.sb([128, 4, D], BF16, "xn")
        self.hT = K.sb([128, NKC, 512], BF16, "hT")
        self.t1 = K.sb([128, 512], F32, "t1")
        self.t2 = K.sb([128, 512], F32, "t2")
        self.t3 = K.sb([128, 512], F32, "t3")
        self.bst = K.sb([128, 6], F32, "bst")
        self.bag = K.sb([128, 2], F32, "bag")
        self.uT = K.sb([128, 4, 512], BF16, "uT")
        self.vh = K.sb([128, 4, 512], BF16, "vh")
        self.stg = [K.sb([128, 512], BF16, "stg%d" % i) for i in range(2)]
        self.stgi = 0
        self.stgv = [K.sb([128, 768], BF16, "stgv%d" % i) for i in range(2)]
        self.stgvi = 0
        self.gqT = K.sb([128, 4, 512], F32, "gqT")
        self.gkT = K.sb([128, 4, 512], F32, "gkT")
        self.gk_tm = K.sb([128, 4, 384], F32, "gk_tm")
        self.gv_tm = K.sb([128, 4, 768], BF16, "gv_tm")
        self.gcT = K.sb([16, 512], F32, "gcT")
        self.sp_ = K.sb([128, 384], F32, "sp")
        self.erb = K.sb([128, 384], F32, "erb")
        self.khat = K.sb([128, 384], BF16, "khat")
        self.e1 = K.sb([128, 128], F32, "e1")
        self.e2 = K.sb([128, 128], F32, "e2")
        self.ecol = K.sb([128, 4, 4], F32, "ecol")
        self.qtT = K.sb([128, 4, 128], BF16, "qtT")
        self.ktT = K.sb([128, 4, 128], BF16, "ktT")
        self.AT = K.sb([128, 4, 128], BF16, "AT")
        self.qfst = K.sb([128, 4, 512], BF16, "qfst")
        self.olst = [K.sb([128, 768], F32, "olst%d" % i) for i in range(2)]
        self.olsti = 0
        self.S = K.sb([128, 4, 192], F32, "S")
        self.Sb = K.sb([128, 4, 192], BF16, "Sb")
        self.pcum = K.sb([128, 4], F32, "pcum")
        self.stst = K.sb([128, 4, 193], F32, "stst")


    def alloc_p2(self, last):
        K = self.K
        self.x4 = [K.sb([128, D], F32, "x4_%d" % i) for i in range(4)]
        self.actin = K.sb([128, NKC, 512], BF16, "actin")
        self.mix_a = Buf(K, self.actin.t[:, 0:4, :], "mix_a")
        self.mix_b = Buf(K, self.actin.t[:, 4:10, :], "mix_b")
        self.mix_c = Buf(K, self.actin.t[:, 10:16, :], "mix_c")
        self.scr16 = K.sb([128, 8192], BF16, "scr16")
        self.xn = Buf(K, self.scr16.t[:, :].rearrange("p (a b) -> p a b", a=4), "xn2")
        self.actT = Buf(K, self.scr16.t[:, :].rearrange("p (a b) -> p a b", a=16), "actT")
        self.ssq = K.sb([128, 4], F32, "ssq2")
        self.rstd = K.sb([128, 4], F32, "rstd2")
        self.qTh = [K.sb([128, 512], BF16, "qTh%d" % i) for i in range(2)]
        self.kTh = [K.sb([128, 512], BF16, "kTh%d" % i) for i in range(3)]
        self.Vh = [K.sb([128, 4, 128], BF16, "Vh%d" % i) for i in range(3)]
        self.Pt = [K.sb([128, 512], BF16, "Pt%d" % i) for i in range(4)]
        self.ft = [K.sb([128, 512], F32, "ft%d" % i) for i in range(3)]
        self.flagones = K.sb([128, 8, 128], BF16, "flagones")
        self.olt = [K.sb([128, 768], F32, "olt%d" % i) for i in range(2)]
        self.srt = [K.sb([128, 768], BF16, "srt%d" % i) for i in range(2)]
        self.qfT = K.sb([128, 4, 512], BF16, "qfT")
        self.of = K.sb([128, 768], F32, "of")
        self.c_all = K.sb([128, 4, 768], BF16, "c_all")
        self.T = K.sb([128, 4, 192], F32, "T")
        self.Sel = K.sb([128, 4, 192], F32, "Sel")
        self.Selb = K.sb([128, 4, 192], BF16, "Selb")
        self.blk = [K.sb([128, 4, 193], F32, "blk%d" % i) for i in range(2)]
        self.t1 = K.sb([128, 512], F32, "t1b")
        self.ssq4 = K.sb([128, 4], F32, "ssq4")
        self.fgb = K.sb([128, D], F32, "fgb") if last else None
        self.kvi = 0
        self.pti = 0
        self.fti = 0
        self.oli = 0

    def phase2(self, l, P, last):
        K, nc = self.K, self.nc
        J = self.nslot
        ring = self.ring
        plan = []
        for j in range(J):
            for cb in range(N_WOUT):
                plan.append(self.wchunk(l, OFF_WOUT + cb))
            for kg in range(3):
                nhp = 8 if kg < 2 else 6
                for hp in range(nhp):
                    plan.append(self.wchunk(l, OFF_FIN + 2 * (kg * 8 + hp)))
                    plan.append(self.wchunk(l, OFF_FIN + 2 * (kg * 8 + hp) + 1))
                for cb in range(8):
                    plan.append(self.wchunk(l, OFF_FOUT + cb * 3 + kg))
        ring.set_plan(plan)
        for r in range(8):
            K.op(K.dve, lambda r=r: nc.vector.tensor_scalar(out=self.flagones[:, r, :], in0=self.ones_b[:, :],
                                                           scalar1=self.flag[:, r:r + 1], scalar2=None, op0=ALU.mult),
                 reads=(self.ones_b, self.flag), writes=(self.flagones,))
        if last:
            for q4 in range(4):
                K.dma(K.sp, self.t1, self.t1[0:1, :], self.fgrow, self.fgrow[0:1, q4 * 512:(q4 + 1) * 512])
                self.bcast_row(self.fgb, self.fgb[:, q4 * 512:(q4 + 1) * 512], self.t1, self.t1[0:1, :], 512)
        K.op(K.pool, lambda: nc.gpsimd.memset(self.T[:, :, :], 0.0), writes=(self.T,))
        xsrc = self.x_in if l == 0 else self.xres
        for j in range(J):
            ring.prefetch()
            for tt in range(4):
                K.dma(K.sp, self.x4[tt], self.x4[tt][:, :], xsrc, xsrc[j, tt * 128:(tt + 1) * 128, :])
            K.dma(K.sp, self.mix_a, self.mix_a[:, :, :], self.aT_s, self.aT_s[j].rearrange("g p t -> p g t"))
            if j == J - 1 and not last:
                self.prep_layer(l + 1)
            self.attention(l, j, P)
            self.gla_final(l, j, P)
            mixes = [(self.mix_a, 0, 4), (self.mix_b, 4, 10), (self.mix_c, 10, 16)]
            for cb in range(8):
                s_ = ring.take()
                w = s_[:, :].rearrange("p (k c) -> p k c", k=16)
                for tt in range(4):
                    bk = self.bank()
                    for kc in range(NKC):
                        mb = [m for m in mixes if m[1] <= kc < m[2]][0]
                        K.mm(bk, bk[:, 0:256], mb[0], self.actin[:, kc, tt * 128:(tt + 1) * 128], s_, w[:, kc, :],
                             kc == 0, kc == NKC - 1)
                    x = self.x4[tt]
                    K.op(K.dve, lambda x=x, cb=cb, bk=bk: nc.vector.tensor_tensor(
                        out=x[:, cb * 256:(cb + 1) * 256], in0=x[:, cb * 256:(cb + 1) * 256], in1=bk[:, 0:256],
                        op=ALU.add), reads=(x, bk), writes=(x,))
            self.actin.link_from([self.mix_a, self.mix_b, self.mix_c])
            self.xn.link_from([self.actT])
            self.norm_from_sbuf(self.x4, P["g2"], self.actin)
            hT = self.actin
            self.actT.link_from([self.xn])
            for kg in range(3):
                nhp = 8 if kg < 2 else 6
                for hp in range(nhp):
                    sg = ring.take()
                    su = ring.take()
                    wg = sg[:, :].rearrange("p (k c) -> p k c", k=16)
                    wu = su[:, :].rearrange("p (k c) -> p k c", k=16)
                    for hh in range(2):
                        hc = hp * 2 + hh
                        bg = self.bank()
                        bu = self.bank()
                        for kc in range(NKC):
                            K.mm(bg, bg[:, :], sg, wg[:, kc, hh * 128:(hh + 1) * 128], hT, hT[:, kc, :], kc == 0,
                                 kc == NKC - 1)
                        for kc in range(NKC):
                            K.mm(bu, bu[:, :], su, wu[:, kc, hh * 128:(hh + 1) * 128], hT, hT[:, kc, :], kc == 0,
                                 kc == NKC - 1)
                        K.actf(self.t1, self.t1[:, :], bg, bg[:, :], AF.Silu)
                        K.op(K.dve, lambda hc=hc, bu=bu: nc.vector.tensor_tensor(out=self.actT[:, hc, :],
                                                                                in0=self.t1[:, :], in1=bu[:, :],
                                                                                op=ALU.mult),
                             reads=(self.t1, bu), writes=(self.actT,))
                nk = 2 * nhp
                for cb in range(8):
                    s_ = ring.take()
                    w = s_[:, :].rearrange("p (k c) -> p k c", k=16)
                    for tt in range(4):
                        bk = self.bank()
                        for kc in range(nk):
                            K.mm(bk, bk[:, 0:256], self.actT, self.actT[:, kc, tt * 128:(tt + 1) * 128], s_,
                                 w[:, kc, :], kc == 0, kc == nk - 1)
                        x = self.x4[tt]
                        K.op(K.dve, lambda x=x, cb=cb, bk=bk: nc.vector.tensor_tensor(
                            out=x[:, cb * 256:(cb + 1) * 256], in0=x[:, cb * 256:(cb + 1) * 256], in1=bk[:, 0:256],
                            op=ALU.add), reads=(x, bk), writes=(x,))
            self.mix_a.link_from([self.actin])
            self.mix_b.link_from([self.actin])
            self.mix_c.link_from([self.actin])
            self.xn.link_from([self.actT])
            for tt in range(4):
                x = self.x4[tt]
                if not last:
                    K.dma(K.sp, self.xres, self.xres[j, tt * 128:(tt + 1) * 128, :], x, x[:, :], sem_b=x, acc=True)
                else:
                    K.op(K.act, lambda x=x, tt=tt: nc.scalar.activation(out=self.xn[:, tt, :], in_=x[:, :],
                                                                      func=AF.Square,
                                                                      accum_out=self.ssq[:, tt:tt + 1]),
                         reads=(x,), writes=(self.xn, self.ssq))
                    self.rms_rstd(self.rstd, self.rstd[:, tt:tt + 1], self.ssq, self.ssq[:, tt:tt + 1], D)
                    K.op(K.dve, lambda x=x, tt=tt: nc.vector.scalar_tensor_tensor(
                        out=x[:, :], in0=x[:, :], scalar=self.rstd[:, tt:tt + 1], in1=self.fgb[:, :], op0=ALU.mult,
                        op1=ALU.mult), reads=(x, self.rstd, self.fgb), writes=(x,))
                    K.dma(K.sp, self.out, self.out[j, tt * 128:(tt + 1) * 128, :], x, x[:, :], sem_b=x, acc=True)

    def norm_from_sbuf(self, xt, gcol, hT):
        K, nc = self.K, self.nc
        for tt in range(4):
            x = xt[tt]
            K.op(K.act, lambda x=x, tt=tt: nc.scalar.activation(out=self.xn[:, tt, :], in_=x[:, :], func=AF.Square,
                                                             accum_out=self.ssq[:, tt:tt + 1]),
                 reads=(x,), writes=(self.xn, self.ssq))
            self.rms_rstd(self.rstd, self.rstd[:, tt:tt + 1], self.ssq, self.ssq[:, tt:tt + 1], D)
            K.op(K.dve, lambda x=x, tt=tt: nc.vector.tensor_scalar(out=self.xn[:, tt, :], in0=x[:, :],
                                                                  scalar1=self.rstd[:, tt:tt + 1], scalar2=None,
                                                                  op0=ALU.mult),
                 reads=(x, self.rstd), writes=(self.xn,))
        for kc in range(NKC):
            bk, bv = self.bbank()
            for tt in range(4):
                K.tr(bk, bv[:, tt * 128:(tt + 1) * 128], self.xn, self.xn[:, tt, kc * 128:(kc + 1) * 128],
                     self.ident, self.ident[:, :], mark=(tt == 3))
            K.op(K.act, lambda kc=kc, bk=bk, bv=bv: nc.scalar.activation(out=hT[:, kc, :], in_=bv[:, 0:512],
                                                                        func=AF.Copy, scale=gcol[:, kc:kc + 1]),
                 reads=(bk, gcol), writes=(hT,))

    def attention(self, l, j, P):
        K, nc = self.K, self.nc
        J = self.nslot
        steps = []
        for jp in range(j):
            for r in range(8):
                steps.append(("full", r, jp))
        for r in range(8):
            steps.append(("flag", r, j))
        steps.append(("diag", None, j))
        o1, o2, l1, l2 = self.accb
        nst = len(steps)
        its = [(si, ks) for si in range(nst) for ks in range(4)]
        for h in range(6):
            qb = self.qTh[h % 2]
            K.dma(K.sp, qb, qb[:, :], self.qT_s, self.qT_s[j, h])
            kvb = {}

            def load_step(si):
                kind, r, jp = steps[si]
                kb = self.kTh[self.kvi % 3]
                vb = self.Vh[self.kvi % 3]
                self.kvi += 1
                if kind == "diag":
                    kr = (jp * 6 + h) * 128
                    K.dma(K.sp, kb, kb[:, :], self.kT_loc, self.kT_loc[kr:kr + 128, :])
                    vr = (jp * 6 + h) * 512
                    K.dma(K.sp, vb, vb[:, :, :], self.v_loc,
                          self.v_loc[vr:vr + 512, :].rearrange("(ks p) e -> p ks e", p=128))
                else:
                    kr = ((r * J + jp) * 6 + h) * 128
                    K.dma(K.sp, kb, kb[:, :], self.kT_all, self.kT_all[kr:kr + 128, :])
                    vr = ((r * J + jp) * 6 + h) * 512
                    K.dma(K.sp, vb, vb[:, :, :], self.v_all,
                          self.v_all[vr:vr + 512, :].rearrange("(ks p) e -> p ks e", p=128))
                if kind == "flag":
                    K.op(K.dve, lambda vb=vb, r=r: nc.vector.tensor_scalar(
                        out=vb[:, :, :], in0=vb[:, :, :], scalar1=self.flag[:, r:r + 1], scalar2=None,
                        op0=ALU.mult), reads=(vb, self.flag), writes=(vb,))
                kvb[si] = (kb, vb)

            def emit_s(n):
                si, ks = its[n]
                if ks == 0:
                    load_step(si)
                kb, vb = kvb[si]
                ksl = slice(ks * 128, (ks + 1) * 128)
                s1 = self.sbank()
                s2 = self.sbank()
                K.mm(s1, s1[:, :], kb, kb[0:64, ksl], qb, qb[0:64, :], True, True)
                K.mm(s2, s2[:, :], kb, kb[64:128, ksl], qb, qb[64:128, :], True, True)
                return s1, s2

            pend = emit_s(0)
            for n in range(len(its)):
                si, ks = its[n]
                kind, r, jp = steps[si]
                kb, vb = kvb[si]
                s1, s2 = pend
                if n + 1 < len(its):
                    pend = emit_s(n + 1)
                first = (n == 0)
                lastm = (n == len(its) - 1)
                p1 = self.Pt[self.pti % 4]
                p2 = self.Pt[(self.pti + 1) % 4]
                self.pti += 2
                K.actf(p1, p1[:, :], s1, s1[:, :], AF.Exp)
                K.actf(p2, p2[:, :], s2, s2[:, :], AF.Exp)
                if kind == "diag":
                    for pp in (p1, p2):
                        K.op(K.dve, lambda pp=pp, ks=ks: nc.vector.tensor_tensor(
                            out=pp[:, :], in0=pp[:, :], in1=self.diag[:, ks, :], op=ALU.mult),
                             reads=(pp, self.diag), writes=(pp,))
                if kind == "flag":
                    onesb, onesap = self.flagones, self.flagones[:, r, :]
                else:
                    onesb, onesap = self.ones_b, self.ones_b[:, :]
                K.mm(o1, o1[:, :], vb, vb[:, ks, :], p1, p1[:, :], first, lastm)
                K.mm(l1, l1[:, :], onesb, onesap, p1, p1[:, :], first, lastm)
                K.mm(o2, o2[:, :], vb, vb[:, ks, :], p2, p2[:, :], first, lastm)
                K.mm(l2, l2[:, :], onesb, onesap, p2, p2[:, :], first, lastm)
            fa = self.ft[0]
            fb = self.ft[1]
            fc = self.ft[2]
            K.op(K.dve, lambda: nc.vector.reciprocal(out=fa[:, :], in_=l1[:, :]), reads=(l1,), writes=(fa,))
            K.op(K.dve, lambda: nc.vector.tensor_tensor(out=fa[:, :], in0=fa[:, :], in1=o1[:, :], op=ALU.mult),
                 reads=(fa, o1), writes=(fa,))
            K.op(K.dve, lambda: nc.vector.reciprocal(out=fb[:, :], in_=l2[:, :]), reads=(l2,), writes=(fb,))
            K.op(K.dve, lambda: nc.vector.tensor_tensor(out=fb[:, :], in0=fb[:, :], in1=o2[:, :], op=ALU.mult),
                 reads=(fb, o2), writes=(fb,))
            K.op(K.dve, lambda: nc.vector.scalar_tensor_tensor(out=fa[:, :], in0=fb[:, :], scalar=P["nlam"][:, 0:1],
                                                              in1=fa[:, :], op0=ALU.mult, op1=ALU.add),
                 reads=(fa, fb, P["nlam"]), writes=(fa,))
            K.actf(fc, fc[:, :], fa, fa[:, :], AF.Square)
            bm = self.sbank()
            K.mm(bm, bm[:, :], self.ones_f, self.ones_f[:, 0:128], fc, fc[:, :], True, True)
            K.actf(fb, fb[:, :], bm, bm[:, :], AF.Sqrt, scale=1.0 / 128.0, bias=EPS)
            K.op(K.dve, lambda: nc.vector.reciprocal(out=fb[:, :], in_=fb[:, :]), reads=(fb,), writes=(fb,))
            K.op(K.dve, lambda h=h: nc.vector.scalar_tensor_tensor(out=self.mix_b[:, h, :], in0=fa[:, :],
                                                                  scalar=P["dngs"][:, 0:1], in1=fb[:, :],
                                                                  op0=ALU.mult, op1=ALU.mult),
                 reads=(fa, fb, P["dngs"]), writes=(self.mix_b,))

    def gla_final(self, l, j, P):
        K, nc = self.K, self.nc
        J = self.nslot
        T, Sel, Selb = self.T, self.Sel, self.Selb
        K.op(K.pool, lambda: nc.gpsimd.memset(Sel[:, :, :], 0.0), writes=(Sel,))
        for r in range(8):
            bb_ = self.blk[r % 2]
            r0 = (r * J + j) * 384
            K.dma(K.sp, bb_, bb_[0:96, :, :], self.st_all,
                  self.st_all[r0:r0 + 384, :].rearrange("(h d) e -> d h e", h=4))
            K.op(K.dve, lambda r=r: nc.vector.scalar_tensor_tensor(
                out=Sel[0:96, :, :], in0=T[0:96, :, :], scalar=self.sel[0:96, r:r + 1], in1=Sel[0:96, :, :],
                op0=ALU.mult, op1=ALU.add), reads=(T, self.sel, Sel), writes=(Sel,))
            for h in range(4):
                K.op(K.dve, lambda h=h, bb_=bb_: nc.vector.scalar_tensor_tensor(
                    out=T[0:96, h, :], in0=T[0:96, h, :], scalar=bb_[0:96, h, 192:193], in1=bb_[0:96, h, 0:192],
                    op0=ALU.mult, op1=ALU.add), reads=(T, bb_), writes=(T,))
        K.op(K.dve, lambda: nc.vector.tensor_copy(out=Selb[0:96, :, :], in_=Sel[0:96, :, :]), reads=(Sel,),
             writes=(Selb,))
        K.dma(K.sp, self.qfT, self.qfT[0:96, :, :], self.qf_s, self.qf_s[j].rearrange("h d t -> d h t"))
        for tt in range(4):
            tsl = slice(tt * 128, (tt + 1) * 128)
            ol = self.olt[self.oli % 2]
            sr = self.srt[self.oli % 2]
            self.oli += 1
            K.dma(K.sp, ol, ol[:, :], self.ol_s, self.ol_s[j, tsl, :])
            K.dma(K.sp, sr, sr[:, :], self.sr_s, self.sr_s[j, tsl, :])
            for half in range(2):
                bo = self.bank()
                for hh in range(2):
                    h = half * 2 + hh
                    K.mm(bo, bo[:, hh * 192:(hh + 1) * 192], self.qfT, self.qfT[0:96, h, tsl], Selb, Selb[0:96, h, :],
                         True, True)
                K.op(K.dve, lambda half=half, bo=bo, ol=ol: nc.vector.tensor_tensor(
                    out=self.of[:, half * 384:(half + 1) * 384], in0=ol[:, half * 384:(half + 1) * 384],
                    in1=bo[:, 0:384], op=ALU.add), reads=(ol, bo), writes=(self.of,))
            for h in range(4):
                hs = slice(h * 192, (h + 1) * 192)
                K.op(K.act, lambda h=h, hs=hs: nc.scalar.activation(out=self.t1[:, 0:192], in_=self.of[:, hs],
                                                                   func=AF.Square,
                                                                   accum_out=self.ssq4[:, h:h + 1]),
                     reads=(self.of,), writes=(self.t1, self.ssq4))
            self.rms_rstd(self.ssq4, self.ssq4[:, 0:4], self.ssq4, self.ssq4[:, 0:4], 192)
            for h in range(4):
                hs = slice(h * 192, (h + 1) * 192)
                K.op(K.dve, lambda h=h, hs=hs, sr=sr, tt=tt: nc.vector.scalar_tensor_tensor(
                    out=self.c_all[:, tt, hs], in0=self.of[:, hs], scalar=self.ssq4[:, h:h + 1], in1=sr[:, hs],
                    op0=ALU.mult, op1=ALU.mult), reads=(self.of, self.ssq4, sr), writes=(self.c_all,))
        for k in range(6):
            bk, bv = self.bbank()
            for tt in range(4):
                K.tr(bk, bv[:, tt * 128:(tt + 1) * 128], self.c_all, self.c_all[:, tt, k * 128:(k + 1) * 128],
                     self.ident, self.ident[:, :], mark=(tt == 3))
            K.op(K.act, lambda k=k, bk=bk, bv=bv: nc.scalar.activation(out=self.mix_c[:, k, :], in_=bv[:, 0:512],
                                                                      func=AF.Copy, scale=P["gng"][:, k:k + 1]),
                 reads=(bk, P["gng"]), writes=(self.mix_c,))

def chunk_rows(W, k_pad=None):
    Kd, N = W.shape
    nkg = Kd // (128 * 16)
    ncb = N // 256
    return W.reshape(nkg, 16, 128, ncb, 256).transpose(3, 0, 2, 1, 4)


def layer_chunks(w_in, w_out, w_ffn_in, w_ffn_out):
    out = np.zeros((N_CHUNK, 128, 16, 256), np.float32)
    Wp = np.zeros((D, N_WIN * 256), np.float32)
    Wp[:, 0:3328] = w_in[:, 0:3328]
    for name, c0, cb0 in (("gq", 3328, CB_GQ), ("gk", 3712, CB_GK)):
        for h in range(4):
            dst = (cb0 + h // 2) * 256 + (h % 2) * 96
            Wp[:, dst:dst + 96] = w_in[:, c0 + h * 96:c0 + (h + 1) * 96]
    Wp[:, CB_GV * 256:CB_GV * 256 + 768] = w_in[:, 4096:4864]
    Wp[:, CB_GR * 256:CB_GR * 256 + 768] = w_in[:, 4864:5632]
    Wp[:, CB_GC * 256:CB_GC * 256 + 16] = w_in[:, 5632:5648]
    out[OFF_WIN:OFF_WIN + N_WIN] = chunk_rows(Wp)[:, 0]
    out[OFF_WOUT:OFF_WOUT + N_WOUT] = chunk_rows(w_out)[:, 0]
    fi = chunk_rows(w_ffn_in)[:, 0]
    for hp in range(22):
        out[OFF_FIN + 2 * hp] = fi[hp]
        out[OFF_FIN + 2 * hp + 1] = fi[22 + hp]
    Fo = np.zeros((48 * 128, D), np.float32)
    Fo[:FFN] = w_ffn_out
    fo = chunk_rows(Fo)
    out[OFF_FOUT:OFF_FOUT + N_FOUT] = fo.reshape(24, 128, 16, 256)
    return out.reshape(N_CHUNK, 128, CH)


def host_consts(core):
    c = {}
    c["c_ident"] = np.eye(128, dtype=np.float32).astype(ml_dtypes.bfloat16)
    c["c_identf"] = np.eye(128, dtype=np.float32)
    s = np.arange(128)[:, None]
    t = np.arange(128)[None, :]
    c["c_uneg"] = np.where(s <= t, -1.0 / 16.0, 0.0).astype(np.float32)
    c["c_lneg"] = np.where(s > t, -1.0 / 16.0, 0.0).astype(np.float32)
    c["c_masku"] = (s <= t).astype(np.float32)
    key = (np.arange(4)[None, :, None] * 128 + np.arange(128)[:, None, None])
    qq = np.arange(512)[None, None, :]
    c["c_diag"] = (key <= qq).astype(np.float32).astype(ml_dtypes.bfloat16)
    r = np.arange(8)
    c["c_flag"] = np.broadcast_to((r < core).astype(np.float32)[None, :], (128, 8)).copy()
    c["c_sel"] = np.broadcast_to((r == core).astype(np.float32)[None, :], (128, 8)).copy()
    return c


def host_inputs(inp, depth, nslot):
    f = lambda a: np.ascontiguousarray(np.asarray(a, dtype=np.float32))
    x = f(inp["x"]).reshape(-1, D)
    nblk = x.shape[0] // BLK
    assert nblk == nslot * NCORES
    xb = x.reshape(nslot, NCORES, BLK, D)
    chunks = [layer_chunks(f(inp["w_in"][l]), f(inp["w_out"][l]), f(inp["w_ffn_in"][l]), f(inp["w_ffn_out"][l]))
              for l in range(depth)]
    common = {
        "g1col": np.stack([f(inp["norm1_g"][l]).reshape(NKC, 128).T for l in range(depth)]),
        "g2col": np.stack([f(inp["norm2_g"][l]).reshape(NKC, 128).T for l in range(depth)]),
        "lngcol": np.stack([f(inp["gmlp_ln_g"][l]).reshape(4, 128).T for l in range(depth)]),
        "swT": np.stack([f(inp["spatial_w"][l]).transpose(2, 0, 1) for l in range(depth)]),
        "sbrow": np.stack([f(inp["spatial_b"][l]).reshape(1, 512) for l in range(depth)]),
        "lamrow": np.stack([f(inp["diff_lambdas"][l]).reshape(1, 256) for l in range(depth)]),
        "dngcol": np.stack([f(inp["diff_norm_g"][l]).reshape(128, 1) for l in range(depth)]),
        "w2": np.stack([f(inp["gla_gate_w2"][l]) for l in range(depth)]),
        "gbrow": np.stack([f(inp["gla_gate_b"][l]).reshape(1, 384) for l in range(depth)]),
        "gngcol": np.stack([np.tile(f(inp["gla_norm_g"][l]), 4).reshape(6, 128).T for l in range(depth)]),
        "fgrow": f(inp["final_g"]).reshape(1, D),
    }
    common = {k: np.ascontiguousarray(v) for k, v in common.items()}
    maps = []
    for c in range(NCORES):
        m = dict(common)
        m["x"] = np.ascontiguousarray(xb[:, c])
        m["wsh"] = np.ascontiguousarray(np.stack([chunks[l][c * CPR:(c + 1) * CPR] for l in range(depth)]))
        m.update(host_consts(c))
        maps.append(m)
    return maps


def finish(P):
    K = P.K
    for b in K.allbufs:
        if b.last_dma is not None:
            K.sp.wait(b.last_dma)
    if K.ccsem.cnt:
        K.sp.wait(Tok(K.ccsem, K.ccsem.cnt))
    for e in (K.pe, K.act, K.dve, K.pool):
        if e.sem.cnt:
            K.sp.wait(Tok(e.sem, e.sem.cnt))


def build(depth, nslot, debug=False, stop_after=None):
    P = Prog(depth, nslot, debug=debug)
    with P.es:
        P.K = Kern(P.nc, P.es)
        K = P.K
        P.declare()
        P.setup_psum()
        P.load_consts()
        P.alloc_layer_params()
        P.ring = Ring(K, 6)
        P.prep_weights()
        for l in range(depth):
            if stop_after == "prep":
                break
            last = (l == depth - 1)
            prm = P.load_layer_params(l)
            with ExitStack() as sc:
                K.scope = sc
                P.alloc_p1()
                P.phase1(l, prm)
                K.barrier()
                K.scope = None
                K.end_scope()
            if stop_after == "p1":
                break
            K.allgather(P.kT_loc, P.kT_loc[:, :], P.kT_all, P.kT_all[:, :])
            K.allgather(P.v_loc, P.v_loc[:, :], P.v_all, P.v_all[:, :])
            K.allgather(P.st_loc, P.st_loc[:, :], P.st_all, P.st_all[:, :])
            with ExitStack() as sc:
                K.scope = sc
                P.alloc_p2(last)
                P.phase2(l, prm, last)
                K.barrier()
                K.scope = None
                K.end_scope()
            K.epoch()
        finish(P)
    return P


_CACHE = {}


def kernel(**inputs):
    depth = int(np.asarray(inputs["w_in"]).shape[0])
    seq = int(np.asarray(inputs["x"]).shape[1])
    nslot = seq // (BLK * NCORES)
    key = (depth, nslot)
    if key not in _CACHE:
        _CACHE[key] = build(depth, nslot)
    P = _CACHE[key]
    maps = host_inputs(inputs, depth, nslot)
    res = run_bass_kernel_spmd(P.nc, maps, core_ids=list(range(NCORES)))
    out = np.zeros((nslot, NCORES, BLK, D), np.float32)
    for c in range(NCORES):
        out[:, c] = np.asarray(res.results[c]["out"]).reshape(nslot, BLK, D)
    return out.reshape(1, seq, D)
```
